# Optimizing a Trainium2 kernel written in Bass

```python
import jax, jax.numpy as jnp
from jax import lax
import numpy as np

D_MODEL = 1024
BATCH = 32
SEQ = 2048
DEPTH = 1
DEC_BATCH = 16
DEC_SEQ = 2048
PAST_LEN = 128

N_MEM = 256
GRID_W = 64
HEAD_DIM = 64
A_GROUPS = ((128, 1), (512, 4), (2048, 16))
A_N_GROUPS = 3
A_HEADS = 8
A_WIDTH = A_HEADS * HEAD_DIM
B_HEADS = 8
B_WIDTH = B_HEADS * HEAD_DIM
NA_ROWS = 8
NA_COLS = 16
NA_Q_COLS = 16
NA_K_COLS = 32
M_HEADS = 4
M_HEAD_DIM = 128
M_WIDTH = M_HEADS * M_HEAD_DIM
N_BRANCH = 3
SPLITS = (A_N_GROUPS * A_WIDTH, A_N_GROUPS * A_WIDTH, A_N_GROUPS * A_WIDTH, A_WIDTH,
          B_WIDTH, B_WIDTH, B_WIDTH, B_WIDTH, M_WIDTH, M_WIDTH, N_BRANCH * D_MODEL)
D_IN = sum(SPLITS)
RMS_EPS = 1e-6
NEG_INF = -1e30

kernel_name = 'hybrid_dilated_neighbourhood_memory_encoder'


def _rmsnorm(x, g):
    x32 = x.astype(jnp.float32)
    y = x32 * lax.rsqrt(jnp.mean(x32 * x32, axis=-1, keepdims=True) + RMS_EPS)
    return (y * g.astype(jnp.float32)).astype(x.dtype)


def _dilated_group(q, k, v, window, dilation, slopes):
    b, s, h, dh = q.shape
    reach = window // (2 * dilation)
    L = s // dilation
    nb = -(-L // reach)
    Lp = nb * reach

    def sub(t):
        return t.reshape(b, L, dilation, h, dh).transpose(0, 2, 1, 3, 4)

    qb = jnp.pad(sub(q), ((0, 0), (0, 0), (0, Lp - L), (0, 0), (0, 0))).reshape(b, dilation, nb, reach, h, dh)

    def kv_blocks(t):
        tp = jnp.pad(sub(t), ((0, 0), (0, 0), (reach, Lp - L + reach), (0, 0), (0, 0)))
        tp = tp.reshape(b, dilation, nb + 2, reach, h, dh)
        return jnp.concatenate([tp[:, :, 0:nb], tp[:, :, 1:nb + 1], tp[:, :, 2:nb + 2]], axis=3)

    kb, vb = kv_blocks(k), kv_blocks(v)
    scores = jnp.einsum('bdnqhe,bdnkhe->bhdnqk', qb, kb) * (dh ** -0.5)
    qi = np.arange(reach)[:, None]
    ki = np.arange(3 * reach)[None, :]
    off = ki - reach - qi
    kpos = np.arange(nb)[:, None, None] * reach + ki[None] - reach
    valid = (np.abs(off) <= reach)[None] & (kpos >= 0) & (kpos < L)
    dist = (np.abs(off) * dilation).astype(np.float32)
    alibi = -slopes[:, None, None] * dist
    sc = jnp.where(valid[None, None, None], scores + alibi[None, :, None, None], NEG_INF)
    m = jnp.max(sc, axis=-1, keepdims=True)
    p = jnp.exp(sc - m)
    l = jnp.sum(p, axis=-1)
    o = jnp.einsum('bhdnqk,bdnkhe->bhdnqe', p, vb)

    def unsub(t):
        rest = t.shape[5:]
        t = t.reshape(b, h, dilation, Lp, *rest)[:, :, :, :L]
        t = t.transpose((0, 3, 2, 1) + tuple(range(4, t.ndim)))
        return t.reshape(b, s, h, *rest)

    return unsub(m[..., 0]), unsub(l), unsub(o)


def _dilated_mixer(q, k, v):
    n_heads = A_N_GROUPS * A_HEADS
    slopes = jnp.exp2(-8.0 * jnp.arange(1, n_heads + 1, dtype=jnp.float32) / n_heads).reshape(A_N_GROUPS, A_HEADS)
    ms, ls, os_ = [], [], []
    for g, (window, dilation) in enumerate(A_GROUPS):
        m, l, o = _dilated_group(q[:, :, g], k[:, :, g], v[:, :, g], window, dilation, slopes[g])
        ms.append(m); ls.append(l); os_.append(o)
    mm = jnp.stack(ms)
    w = jnp.exp(mm - jnp.max(mm, axis=0, keepdims=True))
    num = jnp.sum(w[..., None] * jnp.stack(os_), axis=0)
    den = jnp.sum(w * jnp.stack(ls), axis=0)
    return num / den[..., None]


def _neighbourhood_attention(q, k, v, rpb):
    b, s, h, dh = q.shape
    rows = s // GRID_W
    wr = min(NA_ROWS, rows)
    n_cb = GRID_W // NA_Q_COLS
    qc = np.arange(GRID_W).reshape(n_cb, NA_Q_COLS)
    sc = np.clip(qc - NA_COLS // 2, 0, GRID_W - NA_COLS)
    kstart = np.minimum(sc[:, 0], GRID_W - NA_K_COLS)
    kc = kstart[:, None] + np.arange(NA_K_COLS)[None]
    dc = kc[:, None, :] - qc[:, :, None]
    col_valid = (kc[:, None, :] >= sc[:, :, None]) & (kc[:, None, :] < sc[:, :, None] + NA_COLS)
    dc_idx = np.clip(dc + NA_COLS - 1, 0, 2 * NA_COLS - 2)
    col_bias = rpb[:, :, dc_idx]
    kg = k.reshape(b, rows, GRID_W, h, dh)
    vg = v.reshape(b, rows, GRID_W, h, dh)
    qrows = q.reshape(b, rows, n_cb, NA_Q_COLS, h, dh).transpose(1, 0, 2, 3, 4, 5)
    scale = dh ** -0.5

    def row_fn(args):
        r, q_r = args
        r0 = jnp.clip(r - wr // 2, 0, rows - wr)
        k_r = lax.dynamic_slice_in_dim(kg, r0, wr, axis=1)[:, :, kc]
        v_r = lax.dynamic_slice_in_dim(vg, r0, wr, axis=1)[:, :, kc]
        scores = jnp.einsum('bcqhe,bwcmhe->bhcqwm', q_r, k_r) * scale
        dr_idx = r0 + jnp.arange(wr) - r + NA_ROWS - 1
        bias = jnp.take(col_bias, dr_idx, axis=1).transpose(0, 2, 3, 1, 4)
        sc_ = jnp.where(col_valid[None, None, :, :, None, :], scores + bias[None], NEG_INF)
        p = jax.nn.softmax(sc_.reshape(b, h, n_cb, NA_Q_COLS, wr * NA_K_COLS), axis=-1)
        p = p.reshape(b, h, n_cb, NA_Q_COLS, wr, NA_K_COLS)
        return jnp.einsum('bhcqwm,bwcmhe->bcqhe', p, v_r)

    out = lax.map(row_fn, (jnp.arange(rows), qrows))
    return out.transpose(1, 0, 2, 3, 4, 5).reshape(b, s, h, dh)


def _memory_attention(q, mk, mv):
    scores = jnp.einsum('bshe,bnhe->bhsn', q, mk) * (q.shape[-1] ** -0.5)
    p = jax.nn.softmax(scores, axis=-1)
    return jnp.einsum('bhsn,bnhe->bshe', p, mv)


def _layer(x, mem, g_norm, g_mem, w_in, w_mem_kv, rpb, w_pa, w_pb, w_pm, w_out):
    b, s, _ = x.shape
    f32 = jnp.float32
    h = _rmsnorm(x, g_norm)
    proj = h @ w_in
    split_points = np.cumsum(SPLITS)[:-1].tolist()
    aq, ak, av, ag, bq, bk, bv, bg, mq, mg, merge = jnp.split(proj, split_points, axis=-1)

    def heads(t, *hd):
        return t.reshape(b, t.shape[1], *hd).astype(f32)

    a_out = _dilated_mixer(heads(aq, A_N_GROUPS, A_HEADS, HEAD_DIM),
                           heads(ak, A_N_GROUPS, A_HEADS, HEAD_DIM),
                           heads(av, A_N_GROUPS, A_HEADS, HEAD_DIM))
    b_out = _neighbourhood_attention(heads(bq, B_HEADS, HEAD_DIM), heads(bk, B_HEADS, HEAD_DIM),
                                     heads(bv, B_HEADS, HEAD_DIM), rpb.astype(f32))
    mem_h = _rmsnorm(mem, g_mem)
    mk, mv = jnp.split(mem_h @ w_mem_kv, 2, axis=-1)
    m_out = _memory_attention(heads(mq, M_HEADS, M_HEAD_DIM), heads(mk, M_HEADS, M_HEAD_DIM),
                              heads(mv, M_HEADS, M_HEAD_DIM))

    branch_a = (a_out.reshape(b, s, A_WIDTH).astype(x.dtype) * jax.nn.silu(ag)) @ w_pa
    branch_b = (b_out.reshape(b, s, B_WIDTH).astype(x.dtype) * jax.nn.silu(bg)) @ w_pb
    branch_m = (m_out.reshape(b, s, M_WIDTH).astype(x.dtype) * jax.nn.silu(mg)) @ w_pm
    gates = jax.nn.sigmoid(merge).reshape(b, s, N_BRANCH, D_MODEL)
    merged = gates[:, :, 0] * branch_a + gates[:, :, 1] * branch_b + gates[:, :, 2] * branch_m
    return x + merged @ w_out


def setup_inputs(seed: int = 0) -> dict:
    key = jax.random.key(seed)
    ks = jax.random.split(key, 16)
    nrm = jax.random.normal
    f32 = jnp.float32
    return {
        'x_prompt': nrm(ks[0], (BATCH, SEQ, D_MODEL), f32),
        'x_sample': nrm(ks[1], (DEC_BATCH, DEC_SEQ, D_MODEL), f32),
        'mem_prompt': nrm(ks[2], (BATCH, N_MEM, D_MODEL), f32),
        'mem_sample': nrm(ks[3], (DEC_BATCH, N_MEM, D_MODEL), f32),
        'norm_gain': 1.0 + 0.1 * nrm(ks[4], (DEPTH, D_MODEL), f32),
        'mem_norm_gain': 1.0 + 0.1 * nrm(ks[5], (DEPTH, D_MODEL), f32),
        'w_in': nrm(ks[6], (DEPTH, D_MODEL, D_IN), f32) * D_MODEL ** -0.5,
        'w_mem_kv': nrm(ks[7], (DEPTH, D_MODEL, 2 * M_WIDTH), f32) * D_MODEL ** -0.5,
        'rpb': 0.5 * nrm(ks[8], (DEPTH, B_HEADS, 2 * NA_ROWS - 1, 2 * NA_COLS - 1), f32),
        'w_proj_a': nrm(ks[9], (DEPTH, A_WIDTH, D_MODEL), f32) * A_WIDTH ** -0.5,
        'w_proj_b': nrm(ks[10], (DEPTH, B_WIDTH, D_MODEL), f32) * B_WIDTH ** -0.5,
        'w_proj_m': nrm(ks[11], (DEPTH, M_WIDTH, D_MODEL), f32) * M_WIDTH ** -0.5,
        'w_out': nrm(ks[12], (DEPTH, D_MODEL, D_MODEL), f32) * D_MODEL ** -0.5,
        'final_norm_gain': 1.0 + 0.1 * nrm(ks[13], (D_MODEL,), f32),
    }


def reference(x_prompt, x_sample, mem_prompt, mem_sample, norm_gain, mem_norm_gain, w_in, w_mem_kv,
              rpb, w_proj_a, w_proj_b, w_proj_m, w_out, final_norm_gain):
    def trunk(x, mem):
        for i in range(DEPTH):
            x = _layer(x, mem, norm_gain[i], mem_norm_gain[i], w_in[i], w_mem_kv[i], rpb[i],
                       w_proj_a[i], w_proj_b[i], w_proj_m[i], w_out[i])
        return _rmsnorm(x, final_norm_gain)

    y_prompt = trunk(x_prompt, mem_prompt)
    y_sample = trunk(x_sample, mem_sample)
    return (y_prompt, y_sample)
```

```python
import numpy as np
from contextlib import ExitStack

import concourse.bass as bass
import concourse.mybir as mybir
from concourse.bass_utils import run_bass_kernel_spmd

F32 = mybir.dt.float32
BF16 = mybir.dt.bfloat16
AF = mybir.ActivationFunctionType
ALU = mybir.AluOpType

NCORES = 8
NSEQ = 6
S = 2048
D = 1024
DIN = 11264
OFF = dict(aq=0, ak=1536, av=3072, ag=4608, bq=5120, bk=5632, bv=6144, bg=6656, mq=7168, mg=7680,
           mer=8192)
GROUP_DIL = (1, 4, 16)
RMS_EPS = 1e-6
ENGS = ("sp", "act", "pool", "dve", "pe")


class Op:
    __slots__ = ("eng", "emit", "deps", "signal", "sem", "val", "dma_key", "is_dma", "idx")

    def __init__(self, eng, emit, dma_key):
        self.eng = eng
        self.emit = emit
        self.deps = set()
        self.signal = False
        self.sem = None
        self.val = 0
        self.dma_key = dma_key
        self.is_dma = dma_key is not None


class Planner:
    def __init__(self):
        self.ops = {e: [] for e in ENGS}
        self.last_w = {}
        self.readers = {}

    def add(self, eng, emit, reads=(), writes=(), dma_key=None):
        op = Op(eng, emit, dma_key)
        deps = op.deps
        for k in reads:
            w = self.last_w.get(k)
            if w is not None:
                deps.add(w)
        for k in writes:
            w = self.last_w.get(k)
            if w is not None:
                deps.add(w)
            rd = self.readers.get(k)
            if rd:
                deps.update(rd.values())
        for k in reads:
            rd = self.readers.setdefault(k, {})
            if op.is_dma:
                rd[("dma", id(op))] = op
            else:
                rd[eng] = op
        for k in writes:
            self.last_w[k] = op
            self.readers[k] = {}
        if eng == "pe":
            op.deps = {d for d in deps if d.eng != "pe" or d.is_dma}
        op.idx = len(self.ops[eng])
        self.ops[eng].append(op)
        return op

    def finalize(self):
        for e in ENGS:
            for op in self.ops[e]:
                for d in op.deps:
                    d.signal = True
        cnt = {e: 0 for e in ENGS}
        dcnt = {}
        for e in ENGS:
            for op in self.ops[e]:
                if not op.signal:
                    continue
                if op.is_dma:
                    dcnt[op.dma_key] = dcnt.get(op.dma_key, 0) + 16
                    op.sem = ("dma", op.dma_key)
                    op.val = dcnt[op.dma_key]
                else:
                    cnt[e] += 1
                    op.sem = ("eng", e)
                    op.val = cnt[e]
        return sorted({op.sem for e in ENGS for op in self.ops[e] if op.signal}, key=str)

    def emit_engine(self, e, engine, sems):
        waited = {}
        for op in self.ops[e]:
            need = {}
            for d in op.deps:
                if d.val > need.get(d.sem, 0):
                    need[d.sem] = d.val
            for sk, v in need.items():
                if v > waited.get(sk, 0):
                    engine.wait_ge(sems[sk], v)
                    waited[sk] = v
            ins = op.emit(engine)
            if op.signal:
                ins.then_inc(sems[op.sem], 16 if op.is_dma else 1)


class _DummyPlanner:
    last_w = {}

    def add(self, *a, **k):
        return None


MASK_ENG = "dve"
VEVAC_ENG = "dve"
BVEVAC_ENG = "dve"
LOOK = 3


def MM(out, lhsT, rhs, start, stop):
    return lambda e: e.matmul(out, lhsT=lhsT, rhs=rhs, start=start, stop=stop)


def ACTF(out, in_, func, **kw):
    return lambda e: e.activation(out=out, in_=in_, func=func, **kw)


def TT(out, in0, in1, op):
    return lambda e: e.tensor_tensor(out=out, in0=in0, in1=in1, op=op)


def STT(out, in0, scalar, in1, op0, op1):
    return lambda e: e.scalar_tensor_tensor(out=out, in0=in0, scalar=scalar, in1=in1, op0=op0, op1=op1)


def TS(out, in0, s1, s2, op0, op1):
    return lambda e: e.tensor_scalar(out=out, in0=in0, scalar1=s1, scalar2=s2, op0=op0, op1=op1)


def CP(out, in_):
    return lambda e: e.tensor_copy(out=out, in_=in_)


def DMA(out, in_):
    return lambda e: e.dma_start(out=out, in_=in_)


def RCP(out, in_):
    return lambda e: e.reciprocal_approx_fast(out=out, in_=in_)


def MSET(ap, v):
    return lambda e: e.memset(ap, v)


def _ea_table():
    n_heads = 24
    slopes = np.exp2(-8.0 * np.arange(1, n_heads + 1, dtype=np.float64) / n_heads).reshape(3, 8)
    ea = np.zeros((12, 128, 4, 128), np.float32)
    p = np.arange(128)
    kp = (p % 64)[:, None, None]
    i = np.arange(4)[None, :, None]
    c = np.arange(128)[None, None, :]
    d = 64 * (i - 1) + kp - c
    valid = np.abs(d) <= 64
    for g in range(3):
        for hp in range(4):
            sl = slopes[g, 2 * hp + (p // 64)][:, None, None]
            v = np.exp(-sl * GROUP_DIL[g] * np.abs(d))
            ea[g * 4 + hp] = np.where(valid, v, 0.0).astype(np.float32)
    return ea.reshape(12, 128, 512)


def _bias_b_gather(rpb):
    rpb = np.asarray(rpb, np.float32).reshape(8, 15, 31)
    p = np.arange(128)
    kc = (p % 64)[:, None]
    c = np.arange(64)[None, :]
    sc = np.clip(c - 8, 0, 48)
    valid = (kc >= sc) & (kc < sc + 16)
    dc_idx = np.clip(kc - c + 15, 0, 30)
    out = np.empty((4, 128, 15, 64), np.float32)
    for hp in range(4):
        head = 2 * hp + (p // 64)
        g = rpb[head[:, None, None], np.arange(15)[None, :, None], dc_idx[:, None, :]]
        out[hp] = np.where(valid[:, None, :], g, np.float32(-30000.0))
    return out.reshape(4, 128, 960)


def _const_mats():
    ident = np.eye(128, dtype=np.float32)
    bones = np.zeros((128, 128), np.float32)
    bones[:64, :64] = 1.0
    bones[64:, 64:] = 1.0
    ones = np.ones((128, 128), np.float32)
    return np.stack([ident, bones, ones])


def build(nseq=NSEQ):
    nc = bass.Bass("TRN2", target_bir_lowering=False)
    x_d = nc.dram_tensor("x", [nseq, S, D], F32, kind="ExternalInput").ap()
    mem_d = nc.dram_tensor("mem", [nseq, 256, D], F32, kind="ExternalInput").ap()
    w_in_d = nc.dram_tensor("w_in", [D, DIN], F32, kind="ExternalInput").ap()
    w_kv_d = nc.dram_tensor("w_kv", [D, 1024], F32, kind="ExternalInput").ap()
    w_pa_d = nc.dram_tensor("w_pa", [512, D], F32, kind="ExternalInput").ap()
    w_pb_d = nc.dram_tensor("w_pb", [512, D], F32, kind="ExternalInput").ap()
    w_pm_d = nc.dram_tensor("w_pm", [512, D], F32, kind="ExternalInput").ap()
    w_out_d = nc.dram_tensor("w_out", [D, D], F32, kind="ExternalInput").ap()
    gains_d = nc.dram_tensor("gains", [3, D], F32, kind="ExternalInput").ap()
    ea_d = nc.dram_tensor("ea", [12, 128, 512], F32, kind="ExternalInput").ap()
    bb_d = nc.dram_tensor("bb", [4, 128, 960], F32, kind="ExternalInput").ap()
    cm_d = nc.dram_tensor("cm", [3, 128, 128], F32, kind="ExternalInput").ap()
    g2c_d = nc.dram_tensor("g2c", [128, 8], F32, kind="ExternalInput").ap()
    out_d = nc.dram_tensor("out", [nseq, S, D], F32, kind="ExternalOutput").ap()

    P = None
    collect = True
    es = ExitStack()

    def sb(name, shape, dt):
        return es.enter_context(nc.sbuf_tensor(name, shape, dt))

    NWT, NWP, NXS = 5, 6, 5
    hT = sb("hT", [128, 8, S], BF16)
    gated = [sb("gated%d" % b, [128, 4, S], BF16) for b in range(3)]
    arena = sb("arena", [128, 26624], BF16)
    EAt = [sb("EA%d" % i, [128, 512], BF16) for i in range(2)]
    EBc = sb("EBc", [128, 960], BF16)
    wt = [sb("wt%d" % i, [128, 8, 128], BF16) for i in range(NWT)]
    wp = [sb("wp%d" % i, [128, 4, 128], BF16) for i in range(NWP)]
    xs = [sb("xs%d" % i, [128, D], F32) for i in range(NXS)]
    hb = [sb("hb%d" % i, [128, D], BF16) for i in range(2)]
    gbc = {i: sb("gbc%d" % i, [128, D], F32) for i in (0, 2)}
    g2col = sb("g2col", [128, 8], F32)
    wout = sb("wout", [128, 8, D], BF16)
    mkT = sb("mkT", [128, 4, 256], BF16)
    MV = sb("MV", [128, 2, 512], BF16)
    ident = sb("ident", [128, 128], BF16)
    bones = sb("bones", [128, 128], BF16)
    ones = sb("ones", [128, 128], BF16)
    ssq = sb("ssq", [128, 32], F32)
    ms = sb("ms", [128, 32], F32)
    rstd = sb("rstd", [128, 32], F32)
    mhalf = sb("mhalf", [128, 1], F32)
    dumA = sb("dumA", [128, 2], F32)
    dumD = sb("dumD", [128, 4], F32)

    kblk = arena[:, 0:4096].rearrange("p (b k) -> p b k", k=128)
    vblkT = arena[:, 4096:8192].rearrange("p (b k) -> p b k", k=128)
    BV = arena[:, 8192:12288].rearrange("p (b k) -> p b k", k=128)
    ND = arena[:, 12288:20480].bitcast(F32).rearrange("p (t n) -> p t n", t=2)
    qT = arena[:, 20480:22528]
    PP = [arena[:, 22528 + i * 512: 22528 + (i + 1) * 512] for i in range(4)]
    sg = arena[:, 24576:26624]
    mergedT = arena[:, 0:16384].rearrange("p (c n) -> p c n", c=8)
    th = [arena[:, 16384 + i * 512: 16384 + (i + 1) * 512] for i in range(3)]
    acc = [arena[:, 17920 + i * 1024: 17920 + (i + 1) * 1024].bitcast(F32) for i in range(2)]
    tmpb = [arena[:, 19968 + i * 1024: 19968 + (i + 1) * 1024].bitcast(F32) for i in range(2)]
    memhT = arena[:, 22016:24064].rearrange("p (k n) -> p k n", k=8)

    PS = [es.enter_context(nc.psum_tensor("ps%d" % i, [128, 1024], F32)) for i in range(4)]

    def bank(t, h):
        return PS[t][:, h * 512:(h + 1) * 512]

    pj = [bank(0, 0), bank(0, 1)]
    pr = [bank(1, 0), bank(1, 1), bank(2, 0), bank(2, 1)]
    NPR = 4
    ptb = [bank(3, 0).bitcast(BF16), bank(3, 1).bitcast(BF16)]
    KPJ = [("pj", 0), ("pj", 1)]
    KPR = [("pr", i) for i in range(NPR)]
    KPT = [("ptb", 0), ("ptb", 1)]
    junk = PS[2][:, :]
    KJUNK = [KPR[2], KPR[3]]

    GU = [("ar_u", "act"), ("ar_u", "dve"), ("ar_u", "pool")]
    GT = [("ar_t", "act"), ("ar_t", "dve"), ("ar_t", "pool")]
    KID, KBONES, KONES = ("const", 0), ("const", 1), ("const", 2)

    ctr = {}
    wlist = []

    def gen():
        ctr.clear()

        def nxt(name, n):
            v = ctr.get(name, 0)
            ctr[name] = v + 1
            return v % n

        for i, t in enumerate((ident, bones, ones)):
            P.add("pool", DMA(t[:, :], cm_d[i]), writes=[("const", i)], dma_key=("const", i))
        P.add("pool", MSET(mhalf[:, :], -0.5), writes=[("mhalf",)])
        for i in (0, 2):
            P.add("sp", DMA(gbc[i][:, :], gains_d[i:i + 1, :].broadcast_to([128, D])),
                  writes=[("gbc", i)], dma_key=("gbc", i))
        P.add("sp", DMA(g2col[:, :], g2c_d[:, :]), writes=[("g2col",)], dma_key=("g2col",))
        for kc in range(8):
            P.add("pool", DMA(wout[:, kc, :], w_out_d[kc * 128:(kc + 1) * 128, :]),
                  writes=[("wout", kc)], dma_key=("wout", kc))
        WLOOK = 2

        def _issue_w(i):
            sl = i % NWT
            P.add("pool", DMA(wt[sl][:, :, :], wlist[i].rearrange("(kc p) c -> p kc c", p=128)),
                  writes=[("wt", sl)], dma_key=("wt", sl))

        def load_w(src_ap):
            i = ctr.get("wreq", 0)
            ctr["wreq"] = i + 1
            if collect:
                wlist.append(src_ap)
                return i % NWT
            while ctr.get("wiss", 0) <= min(i + WLOOK, len(wlist) - 1):
                _issue_w(ctr.get("wiss", 0))
                ctr["wiss"] = ctr.get("wiss", 0) + 1
            return i % NWT

        def load_wp(src_ap):
            sl = nxt("wp", NWP)
            P.add("pool", DMA(wp[sl][:, :, :], src_ap.rearrange("(kc p) c -> p kc c", p=128)),
                  writes=[("wp", sl)], dma_key=("wp", sl))
            return sl

        def rstd_col(col, src_key):
            P.add("dve", TS(ms[:, col:col + 1], ssq[:, col:col + 1], 1.0 / D, RMS_EPS, ALU.mult, ALU.add),
                  reads=[("ssq", col)], writes=[("ms", col)])
            P.add("pool", TT(rstd[:, col:col + 1], ms[:, col:col + 1], mhalf[:, 0:1], ALU.pow),
                  reads=[("ms", col), ("mhalf",)], writes=[("rstd", col)])

        def norm_a(src_rows):
            sl = nxt("xs", NXS)
            col = nxt("col", 32)
            P.add("sp", DMA(xs[sl][:, :], src_rows), writes=[("xs", sl)], dma_key=("xs", sl))
            P.add("act", ACTF(junk, xs[sl][:, :], AF.Square, accum_out=ssq[:, col:col + 1]),
                  reads=[("xs", sl)], writes=[("ssq", col)] + KJUNK)
            rstd_col(col, None)
            return sl, col

        def norm_b1(st, gi):
            sl, col = st
            hs = nxt("hb", 2)
            if gi is not None:
                P.add("dve", STT(hb[hs][:, :], xs[sl][:, :], rstd[:, col:col + 1], gbc[gi][:, :], ALU.mult, ALU.mult),
                      reads=[("xs", sl), ("rstd", col), ("gbc", gi)], writes=[("hb", hs)])
            else:
                P.add("dve", TS(hb[hs][:, :], xs[sl][:, :], rstd[:, col:col + 1], None, ALU.mult, ALU.bypass),
                      reads=[("xs", sl), ("rstd", col)], writes=[("hb", hs)])
            return hs

        def norm_b2(hs):
            pt = nxt("ptb", 2)
            for kc in range(8):
                P.add("pe", lambda e, kc=kc, pt=pt, hs=hs: e.transpose(
                    out=ptb[pt][:, kc * 128:(kc + 1) * 128], in_=hb[hs][:, kc * 128:(kc + 1) * 128], identity=ident[:, :]),
                    reads=[("hb", hs), KID], writes=[KPT[pt]])
            return pt

        def norm_b(st, gi):
            return norm_b2(norm_b1(st, gi))

        def norm_c(pt, gi, dstT, dst_cols, dst_keys, extra_r=()):
            src = ptb[pt].rearrange("p (k n) -> p k n", k=8)
            if gi is not None:
                P.add("act", ACTF(dstT[:, :, dst_cols], src, AF.Copy), reads=[KPT[pt]] + list(extra_r), writes=list(dst_keys))
            else:
                P.add("dve", TT(dstT[:, :, dst_cols], src, g2col[:, :].unsqueeze(2).to_broadcast([128, 8, 128]), ALU.mult),
                      reads=[KPT[pt], ("g2col",)] + list(extra_r), writes=list(dst_keys))

        def run_pipes(pipes, n):
            depth = max(len(p) for p in pipes)
            states = [[{} for _ in range(n)] for _ in pipes]
            for i in range(n + depth - 1):
                for pi, p in enumerate(pipes):
                    for k in reversed(range(len(p))):
                        tt = i - k
                        if 0 <= tt < n:
                            p[k](tt, states[pi][tt])

        def inproj_tile(w_sl, tc):
            b = nxt("pj", 2)
            for kc in range(8):
                P.add("pe", MM(pj[b], wt[w_sl][:, kc, :], hT[:, kc, tc * 512:(tc + 1) * 512], kc == 0, kc == 7),
                      reads=[("wt", w_sl), ("hT",)], writes=[KPJ[b]])
            return b

        def fence(phase_keys):
            P.add("act", ACTF(dumA[:, 0:1], mhalf[:, 0:1], AF.Copy), reads=[("mhalf",)], writes=[phase_keys[0]])
            P.add("dve", MSET(dumD[:, 0:1], 0.0), writes=[phase_keys[1]])

        def evac_q(b, tc, dil):
            src = pj[b]
            if dil == 1:
                o = qT[:, tc * 512:(tc + 1) * 512]
                i_ = src
            else:
                n = 512 // dil
                o = qT.rearrange("p (r i) -> p r i", r=dil)[:, :, tc * n:(tc + 1) * n]
                i_ = src.rearrange("p (i r) -> p r i", r=dil)
            P.add("dve", CP(o, i_), reads=[KPJ[b]] + GU, writes=[("qT", tc)])

        def evac_blk(dst, dkey, b, tc, dil, eng="dve"):
            for hh in range(2):
                rows = slice(hh * 64, hh * 64 + 64)
                cs = slice(hh * 64, hh * 64 + 64)
                src = pj[b][rows, :]
                if dil == 1:
                    o = dst[rows, tc * 8:(tc + 1) * 8, cs]
                    i_ = src.rearrange("p (b k) -> p b k", k=64)
                elif dil == 4:
                    o = dst.rearrange("p (r b) k -> p r b k", r=4)[rows, :, 2 * tc:2 * tc + 2, cs]
                    i_ = src.rearrange("p (b k r) -> p r b k", r=4, k=64)
                else:
                    c0 = hh * 64 + 32 * (tc % 2)
                    o = dst.rearrange("p (r b) k -> p r b k", r=16)[rows, :, tc // 2, c0:c0 + 32]
                    i_ = src.rearrange("p (i r) -> p r i", r=16)
                if eng == "dve":
                    P.add("dve", CP(o, i_), reads=[KPJ[b]] + GU, writes=[(dkey, tc, hh)])
                else:
                    P.add("act", ACTF(o, i_, AF.Copy), reads=[KPJ[b]] + GU, writes=[(dkey, tc, hh)])

        KQ = [("qT", tc) for tc in range(4)]
        KK = [("kblk", tc, hh) for tc in range(4) for hh in range(2)]
        KVT = [("vblkT", tc, hh) for tc in range(4) for hh in range(2)]
        KBV = [("BV", q) for q in range(4)]
        KSG = [("sg", tc) for tc in range(4)]

        def build_bv():
            for q in range(4):
                pt = nxt("ptb", 2)
                for j in range(8):
                    blk = q * 8 + j
                    P.add("pe", lambda e, blk=blk, j=j, pt=pt: e.transpose(
                        out=ptb[pt][:, j * 128:(j + 1) * 128], in_=vblkT[:, blk, :], identity=ident[:, :]),
                        reads=KVT + [KID] + GU, writes=[KPT[pt]])
                P.add("act", ACTF(BV[:, q * 8:(q + 1) * 8, :], ptb[pt].rearrange("p (b k) -> p b k", k=128), AF.Copy),
                      reads=[KPT[pt]] + GU, writes=[("BV", q)])

        def v_chunks(vc0, dil):
            box = {}

            def chunk(tc):
                if "w" not in box:
                    box["w"] = load_w(w_in_d[:, vc0:vc0 + 128])
                b = inproj_tile(box["w"], tc)
                evac_blk(vblkT, "vblkT", b, tc, dil, eng=VEVAC_ENG)

            return [lambda tc=tc: chunk(tc) for tc in range(4)]

        def qk_inproj(qc0, kc0, dil):
            for (c0, kind) in ((qc0, "q"), (kc0, "k")):
                if kind == "k":
                    build_bv()
                w_sl = load_w(w_in_d[:, c0:c0 + 128])
                for tc in range(4):
                    b = inproj_tile(w_sl, tc)
                    if kind == "q":
                        evac_q(b, tc, dil)
                        flush_fin(1)
                    else:
                        evac_blk(kblk, "kblk", b, tc, dil)
                        if tc == 0:
                            flush_fin()

        def gate_chunks(c0, tslots=(0, 1)):
            box = {}

            def chunk(tc):
                if "w" not in box:
                    box["w"] = load_w(w_in_d[:, c0:c0 + 128])
                b = inproj_tile(box["w"], tc)
                t = tslots[tc % len(tslots)]
                P.add("act", ACTF(EAt[t][:, :], pj[b], AF.Tanh, scale=0.5), reads=[KPJ[b]], writes=[("EA", t)])
                P.add("dve", STT(sg[:, tc * 512:(tc + 1) * 512], EAt[t][:, :], 1.0, pj[b], ALU.add, ALU.mult),
                      reads=[KPJ[b], ("EA", t)] + GU, writes=[("sg", tc)])

            return [lambda tc=tc: chunk(tc) for tc in range(4)]

        def gate_inproj(c0, tslots=(0, 1)):
            for f in gate_chunks(c0, tslots):
                f()

        def nd_keys():
            return [k for k in P.last_w.keys() if isinstance(k, tuple) and k and k[0] == "ND"]

        def pipeline(st1, st2, look, fillers=(), every=1, pre=(), mid=()):
            n = len(st1)
            fl = list(fillers)
            for f in pre:
                f()
            for i in range(min(look, n)):
                st1[i]()
            for f in mid:
                f()
            for i in range(n):
                if i + look < n:
                    st1[i + look]()
                if fl and i % every == every - 1:
                    fl.pop(0)()
                st2[i]()
            while fl:
                fl.pop(0)()

        def load_ea(g, hp):
            ea = nxt("ea", 2)
            P.add("pool", DMA(EAt[ea][:, :], ea_d[g * 4 + hp]), writes=[("EA", ea)], dma_key=("EA", ea))
            return ea

        def mask_mult(ap, tab, rkeys, wkey):
            P.add(MASK_ENG, TT(ap, ap, tab, ALU.mult), reads=rkeys + GU, writes=[wkey])

        def attn_banded(g, hp, first, ea, pre=(), mid=()):
            dil = GROUP_DIL[g]
            nb = 32 // dil
            st1, st2 = [], []
            for sidx in range(dil):
                for n2 in range(nb // 2):
                    st = {}

                    def s1(sidx=sidx, n2=n2, st=st):
                        B0 = sidx * nb
                        iv = [i for i in range(4) if 0 <= 2 * n2 - 1 + i < nb]
                        i0, i1 = iv[0], iv[-1]
                        sb_ = nxt("pr", NPR)
                        q0 = (B0 + 2 * n2) * 64
                        for i in iv:
                            jl = 2 * n2 - 1 + i
                            P.add("pe", MM(pr[sb_][:, i * 128:(i + 1) * 128], kblk[:, B0 + jl, :], qT[:, q0:q0 + 128], True, True),
                                  reads=KQ + KK + GU, writes=[KPR[sb_]])
                        t = nxt("pp", 4)
                        cs = slice(i0 * 128, (i1 + 1) * 128)
                        P.add("act", ACTF(PP[t][:, cs], pr[sb_][:, cs], AF.Exp, scale=0.125),
                              reads=[KPR[sb_]] + GU, writes=[("PP", t)])
                        mask_mult(PP[t][:, cs], EAt[ea][:, cs], [("PP", t), ("EA", ea)], ("PP", t))
                        st.update(t=t, iv=iv, B0=B0, q0=q0)

                    def s2(sidx=sidx, n2=n2, st=st):
                        t, iv, B0, q0 = st["t"], st["iv"], st["B0"], st["q0"]
                        i0, i1 = iv[0], iv[-1]
                        ob = nxt("pr", NPR)
                        for i in iv:
                            jl = 2 * n2 - 1 + i
                            P.add("pe", MM(pr[ob][:, 0:128], BV[:, B0 + jl, :], PP[t][:, i * 128:(i + 1) * 128], i == i0, i == i1),
                                  reads=[("PP", t)] + KBV + GU, writes=[KPR[ob]])
                        for i in iv:
                            P.add("pe", MM(pr[ob][:, 128:256], bones[:, :], PP[t][:, i * 128:(i + 1) * 128], i == i0, i == i1),
                                  reads=[("PP", t), KBONES] + GU, writes=[KPR[ob]])
                        if dil == 1:
                            dst = ND[:, :, q0:q0 + 128]
                        elif dil == 4:
                            t0 = sidx + 4 * 128 * n2
                            dst = ND[:, :, t0:t0 + 4 * 127 + 1:4]
                        else:
                            dst = ND[:, :, sidx:S:16]
                        src = pr[ob][:, 0:256].rearrange("p (t n) -> p t n", t=2)
                        ndk = [("ND", g, sidx, n2)]
                        if first:
                            P.add("dve", CP(dst, src), reads=[KPR[ob], ("NDfree",)] + GU, writes=ndk)
                        else:
                            P.add("dve", TT(dst, src, dst, ALU.add), reads=[KPR[ob], ("NDall",)] + GU, writes=ndk)

                    st1.append(s1)
                    st2.append(s2)
            pipeline(st1, st2, LOOK, pre=pre, mid=mid)

        def nd_barrier():
            P.add("dve", MSET(dumD[:, 1:2], 0.0), reads=nd_keys(), writes=[("NDall",)])

        pend = []

        def finalize_unit(dst_gated, gkey):
            for h in range(2):
                c = slice(h * 1024, (h + 1) * 1024)
                P.add("act", ACTF(ND[:, 1, c], ND[:, 1, c], AF.Ln), reads=[("NDall",)] + GU, writes=[("NDl", h)])
                P.add("act", ACTF(ND[:, 1, c], ND[:, 1, c], AF.Exp, scale=-1.0), reads=[("NDl", h)] + GU, writes=[("NDr", h)])

            def op_tt(h):
                c = slice(h * 1024, (h + 1) * 1024)
                P.add("dve", TT(ND[:, 0, c], ND[:, 0, c], ND[:, 1, c], ALU.mult), reads=[("NDr", h)] + GU, writes=[("NDn", h)])

            def op_stt(h):
                c = slice(h * 1024, (h + 1) * 1024)
                P.add("dve", STT(dst_gated[:, c], ND[:, 0, c], 0.5, sg[:, c], ALU.mult, ALU.mult),
                      reads=[("NDn", h)] + KSG + GU, writes=[gkey, ("NDg", h)])
                if h == 1:
                    P.add("dve", MSET(dumD[:, 2:3], 0.0), reads=[("NDg", 0), ("NDg", 1)], writes=[("NDfree",)])

            pend.extend([lambda: op_tt(0), lambda: op_stt(0), lambda: op_tt(1), lambda: op_stt(1)])

        def flush_fin(n=None):
            while pend and (n is None or n > 0):
                pend.pop(0)()
                if n is not None:
                    n -= 1

        def load_eb(hp):
            sl = nxt("xs", NXS)
            P.add("sp", DMA(xs[sl][:, 0:960], bb_d[hp]), writes=[("xs", sl)], dma_key=("xs", sl))
            P.add("act", ACTF(EBc[:, :], xs[sl][:, 0:960], AF.Exp), reads=[("xs", sl)], writes=[("EBc",)])

        def attn_nbr(hp, pre=(), mid=()):
            st1, st2 = [], []
            obs = {}
            for r in range(32):
                st = {}

                def s1(r=r, st=st):
                    r0 = min(max(r - 4, 0), 24)
                    sb_ = nxt("pr", NPR)
                    for i in range(8):
                        P.add("pe", MM(pr[sb_][:, i * 64:(i + 1) * 64], kblk[:, r0 + i, :], qT[:, r * 64:(r + 1) * 64], True, True),
                              reads=KQ + KK + GU, writes=[KPR[sb_]])
                    t = nxt("pp", 4)
                    P.add("act", ACTF(PP[t][:, :], pr[sb_], AF.Exp, scale=0.125), reads=[KPR[sb_]] + GU, writes=[("PP", t)])
                    e0 = (r0 - r + 7) * 64
                    mask_mult(PP[t][:, :], EBc[:, e0:e0 + 512], [("PP", t), ("EBc",)], ("PP", t))
                    st.update(t=t, r0=r0)

                def s2(r=r, st=st):
                    t, r0 = st["t"], st["r0"]
                    if r % 4 == 0:
                        obs["ob"] = nxt("pr", NPR)
                    ob = obs["ob"]
                    o0 = (r % 4) * 64
                    for i in range(8):
                        P.add("pe", MM(pr[ob][:, o0:o0 + 64], BV[:, r0 + i, :], PP[t][:, i * 64:(i + 1) * 64], i == 0, i == 7),
                              reads=[("PP", t)] + KBV + GU, writes=[KPR[ob]])
                    for i in range(8):
                        P.add("pe", MM(pr[ob][:, 256 + o0:256 + o0 + 64], bones[:, :], PP[t][:, i * 64:(i + 1) * 64], i == 0, i == 7),
                              reads=[("PP", t), KBONES] + GU, writes=[KPR[ob]])
                    if r % 4 == 3:
                        q0 = (r - 3) * 64
                        P.add("dve", CP(ND[:, :, q0:q0 + 256], pr[ob].rearrange("p (t n) -> p t n", t=2)),
                              reads=[KPR[ob], ("NDfree",)] + GU, writes=[("ND", "b", r)])

                st1.append(s1)
                st2.append(s2)
            pipeline(st1, st2, LOOK, pre=pre, mid=mid)

        def attn_mem(h, fillers=()):
            st1, st2 = [], []
            for qc in range(4):
                st = {}

                def s1(qc=qc, st=st):
                    tl = []
                    for kt in range(2):
                        P.add("pe", MM(pr[kt], mkT[:, h, kt * 128:(kt + 1) * 128], qT[:, qc * 512:(qc + 1) * 512], True, True),
                              reads=KQ + [("mkT", h)] + GU, writes=[KPR[kt]])
                        t = 2 * (qc % 2) + kt
                        tl.append(t)
                        P.add("act", ACTF(PP[t][:, :], pr[kt], AF.Exp, scale=float(128 ** -0.5)),
                              reads=[KPR[kt]] + GU, writes=[("PP", t)])
                    st["tl"] = tl

                def s2(qc=qc, st=st):
                    tl = st["tl"]
                    for kt in range(2):
                        P.add("pe", MM(pr[2], MV[:, kt, h * 128:(h + 1) * 128], PP[tl[kt]][:, :], kt == 0, kt == 1),
                              reads=[("PP", tl[kt]), ("MV",)] + GU, writes=[KPR[2]])
                    for kt in range(2):
                        P.add("pe", MM(pr[3], ones[:, :], PP[tl[kt]][:, :], kt == 0, kt == 1),
                              reads=[("PP", tl[kt]), KONES] + GU, writes=[KPR[3]])
                    for ob in range(2):
                        P.add("dve", CP(ND[:, ob, qc * 512:(qc + 1) * 512], pr[2 + ob]),
                              reads=[KPR[2 + ob], ("NDfree",)] + GU, writes=[("ND", "m", qc, ob)])

                st1.append(s1)
                st2.append(s2)
            pipeline(st1, st2, 1, fillers, 1)

        def mem_branch(s):
            sts = [norm_a(mem_d[s, t * 128:(t + 1) * 128, :]) for t in range(2)]
            pts = [norm_b(sts[t], None) for t in range(2)]
            for t in range(2):
                norm_c(pts[t], None, memhT, slice(t * 128, (t + 1) * 128), [("memhT", t)], extra_r=GT)
            rk = [("memhT", 0), ("memhT", 1)] + GT
            for h in range(4):
                w_sl = load_w(w_kv_d[:, h * 128:(h + 1) * 128])
                b = nxt("pj", 2)
                for kc in range(8):
                    P.add("pe", MM(pj[b][:, 0:256], wt[w_sl][:, kc, :], memhT[:, kc, :], kc == 0, kc == 7),
                          reads=[("wt", w_sl)] + rk, writes=[KPJ[b]])
                P.add("dve", CP(mkT[:, h, :], pj[b][:, 0:256]), reads=[KPJ[b]], writes=[("mkT", h)])
            for h in range(4):
                w_sl = load_w(w_kv_d[:, 512 + h * 128:512 + (h + 1) * 128])
                b = nxt("pj", 2)
                for kt in range(2):
                    for kc in range(8):
                        P.add("pe", MM(pj[b][:, kt * 128:(kt + 1) * 128], memhT[:, kc, kt * 128:(kt + 1) * 128],
                                       wt[w_sl][:, kc, :], kc == 0, kc == 7),
                              reads=[("wt", w_sl)] + rk, writes=[KPJ[b]])
                P.add("dve", CP(MV[:, :, h * 128:(h + 1) * 128], pj[b][:, 0:256].rearrange("p (t n) -> p t n", t=2)),
                      reads=[KPJ[b]], writes=[("MV",)])

        def zero_blocks():
            P.add("pool", MSET(arena[:, 0:8192], 0.0), reads=GU, writes=[GT[2]] + KK + KVT)

        def tail(s):
            fence(GU)
            wps = (w_pa_d, w_pb_d, w_pm_d)
            for c in range(8):
                wg = [load_w(w_in_d[:, OFF["mer"] + b * 1024 + c * 128: OFF["mer"] + b * 1024 + (c + 1) * 128])
                      for b in range(3)]
                if c == 0:
                    wpn = [load_wp(wps[b][:, 0:128]) for b in range(3)]
                wpl = wpn
                if c + 1 < 8:
                    wpn = [load_wp(wps[b][:, (c + 1) * 128:(c + 2) * 128]) for b in range(3)]
                for tc in range(4):
                    a = nxt("acc", 2)
                    for b in range(3):
                        pb_ = inproj_tile(wg[b], tc)
                        t = nxt("th", 3)
                        P.add("act", ACTF(th[t][:, :], pj[pb_], AF.Tanh, scale=0.5), reads=[KPJ[pb_]] + GT, writes=[("th", t)])
                        sb_ = nxt("pr", NPR)
                        for kc in range(4):
                            P.add("pe", MM(pr[sb_], wp[wpl[b]][:, kc, :], gated[b][:, kc, tc * 512:(tc + 1) * 512], kc == 0, kc == 3),
                                  reads=[("wp", wpl[b]), ("gated", b)], writes=[KPR[sb_]])
                        if b == 0:
                            P.add("dve", STT(acc[a][:, :], th[t][:, :], 1.0, pr[sb_], ALU.add, ALU.mult),
                                  reads=[("th", t), KPR[sb_]] + GT, writes=[("acc", a)])
                        else:
                            m = nxt("tmp", 2)
                            P.add("dve", STT(tmpb[m][:, :], th[t][:, :], 1.0, pr[sb_], ALU.add, ALU.mult),
                                  reads=[("th", t), KPR[sb_]] + GT, writes=[("tmp", m)])
                            if b == 1:
                                P.add("dve", TT(acc[a][:, :], acc[a][:, :], tmpb[m][:, :], ALU.add),
                                      reads=[("acc", a), ("tmp", m)] + GT, writes=[("acc", a)])
                            else:
                                P.add("dve", TT(mergedT[:, c, tc * 512:(tc + 1) * 512], acc[a][:, :], tmpb[m][:, :], ALU.add),
                                      reads=[("acc", a), ("tmp", m)] + GT, writes=[("mergedT", c, tc)])

        def final_pipe(s):
            def fb(tt, st):
                sl = nxt("xs", NXS)
                col = nxt("col", 32)
                rows = slice(tt * 128, (tt + 1) * 128)
                P.add("sp", DMA(xs[sl][:, :], x_d[s, rows, :]), writes=[("xs", sl)], dma_key=("xs", sl))
                mk = [("mergedT", c, tt // 4) for c in range(8)]
                for half in range(2):
                    b = nxt("pj", 2)
                    for kc in range(8):
                        P.add("pe", MM(pj[b], mergedT[:, kc, rows], wout[:, kc, half * 512:(half + 1) * 512], kc == 0, kc == 7),
                              reads=mk + [("wout", kc)] + GT, writes=[KPJ[b]])
                    hs = slice(half * 512, (half + 1) * 512)
                    P.add("dve", STT(xs[sl][:, hs], pj[b], 0.5, xs[sl][:, hs], ALU.mult, ALU.add),
                          reads=[KPJ[b], ("xs", sl)], writes=[("xs", sl)])
                P.add("act", ACTF(junk, xs[sl][:, :], AF.Square, accum_out=ssq[:, col:col + 1]),
                      reads=[("xs", sl)], writes=[("ssq", col)] + KJUNK)
                rstd_col(col, None)
                st.update(sl=sl, col=col, rows=rows)

            def fc(tt, st):
                sl, col, rows = st["sl"], st["col"], st["rows"]
                P.add("dve", STT(xs[sl][:, :], xs[sl][:, :], rstd[:, col:col + 1], gbc[2][:, :], ALU.mult, ALU.mult),
                      reads=[("xs", sl), ("rstd", col), ("gbc", 2)], writes=[("xs", sl)])
                P.add("pool", DMA(out_d[s, rows, :], xs[sl][:, :]), reads=[("xs", sl)], writes=[("out", s, tt)],
                      dma_key=("st", sl))

            return [fb, fc]

        def phase_n_pipe(s):
            def a(tt, st):
                st["a"] = norm_a(x_d[s, tt * 128:(tt + 1) * 128, :])

            def b1(tt, st):
                st["hs"] = norm_b1(st["a"], 0)

            def b2(tt, st):
                st["pt"] = norm_b2(st["hs"])

            def c(tt, st):
                norm_c(st["pt"], 0, hT, slice(tt * 128, (tt + 1) * 128), [("hT",)])

            return [a, b1, b2, c]

        def units(s):
            fence(GT)
            zero_blocks()
            ul = []
            for hp in range(4):
                for g in range(3):
                    c = g * 512 + hp * 128
                    ul.append(("A", g, hp, OFF["aq"] + c, OFF["ak"] + c, OFF["av"] + c, GROUP_DIL[g]))
            for hp in range(4):
                c = hp * 128
                ul.append(("B", 0, hp, OFF["bq"] + c, OFF["bk"] + c, OFF["bv"] + c, 1))
            for f in v_chunks(ul[0][5], ul[0][6]):
                f()
            for i, (kind, g, hp, qc0, kc0, vc0, dil) in enumerate(ul):
                if i + 1 < len(ul):
                    vn = v_chunks(ul[i + 1][5], ul[i + 1][6])
                    pre, mid = vn[0:2], vn[2:4]
                else:
                    pre, mid = (), ()
                if kind == "A":
                    ea = load_ea(g, hp)
                    qk_inproj(qc0, kc0, dil)
                    attn_banded(g, hp, g == 0, ea, pre, mid)
                    nd_barrier()
                    if g == 2:
                        gate_inproj(OFF["ag"] + hp * 128)
                        finalize_unit(gated[0][:, hp, :], ("gated", 0))
                else:
                    load_eb(hp)
                    qk_inproj(qc0, kc0, dil)
                    attn_nbr(hp, pre, mid)
                    nd_barrier()
                    gate_inproj(OFF["bg"] + hp * 128)
                    finalize_unit(gated[1][:, hp, :], ("gated", 1))
            for h in range(4):
                w_sl = load_w(w_in_d[:, OFF["mq"] + h * 128: OFF["mq"] + (h + 1) * 128])
                for tc in range(4):
                    b = inproj_tile(w_sl, tc)
                    evac_q(b, tc, 1)
                flush_fin()
                attn_mem(h, gate_chunks(OFF["mg"] + h * 128))
                nd_barrier()
                finalize_unit(gated[2][:, h, :], ("gated", 2))
            flush_fin()

        run_pipes([phase_n_pipe(0)], 16)
        for s in range(nseq):
            mem_branch(s)
            units(s)
            tail(s)
            pipes = [final_pipe(s)]
            if s + 1 < nseq:
                pipes.insert(0, phase_n_pipe(s + 1))
            run_pipes(pipes, 16)
        P.add("sp", lambda e: e.nop(), reads=[("out", s, tt) for s in range(nseq) for tt in range(16)])


    collect = True
    P = _DummyPlanner()
    gen()
    collect = False
    P = Planner()
    gen()

    sem_keys = P.finalize()
    sems = {}
    for i, sk in enumerate(sem_keys):
        sems[sk] = es.enter_context(nc.semaphore("sem%d" % i))
    with nc.Block() as block:
        @block.sync
        def _(e):
            P.emit_engine("sp", e, sems)

        @block.scalar
        def _(e):
            P.emit_engine("act", e, sems)

        @block.gpsimd
        def _(e):
            P.emit_engine("pool", e, sems)

        @block.vector
        def _(e):
            P.emit_engine("dve", e, sems)

        @block.tensor
        def _(e):
            P.emit_engine("pe", e, sems)
    es.close()
    return nc, P


def _shared_inputs(norm_gain, mem_norm_gain, w_in, w_mem_kv, rpb, w_proj_a, w_proj_b, w_proj_m, w_out,
                   final_norm_gain):
    f = lambda a: np.ascontiguousarray(np.asarray(a, dtype=np.float32))
    return {
        "w_in": f(w_in).reshape(D, DIN), "w_kv": f(w_mem_kv).reshape(D, 1024),
        "w_pa": f(w_proj_a).reshape(512, D), "w_pb": f(w_proj_b).reshape(512, D),
        "w_pm": f(w_proj_m).reshape(512, D), "w_out": f(w_out).reshape(D, D),
        "gains": np.stack([f(norm_gain).reshape(D), f(mem_norm_gain).reshape(D), f(final_norm_gain).reshape(D)]),
        "ea": _ea_table(), "bb": _bias_b_gather(rpb), "cm": _const_mats(),
        "g2c": np.ascontiguousarray(f(mem_norm_gain).reshape(8, 128).T),
    }


_CACHE = {}


def kernel(x_prompt, x_sample, mem_prompt, mem_sample, norm_gain, mem_norm_gain, w_in, w_mem_kv, rpb,
           w_proj_a, w_proj_b, w_proj_m, w_out, final_norm_gain):
    f = lambda a: np.ascontiguousarray(np.asarray(a, dtype=np.float32))
    x_prompt, x_sample, mem_prompt, mem_sample = f(x_prompt), f(x_sample), f(mem_prompt), f(mem_sample)
    if "nc" not in _CACHE:
        _CACHE["nc"] = build(NSEQ)[0]
    nc = _CACHE["nc"]
    shared = _shared_inputs(norm_gain, mem_norm_gain, w_in, w_mem_kv, rpb, w_proj_a, w_proj_b, w_proj_m, w_out,
                            final_norm_gain)
    in_maps = []
    for c in range(NCORES):
        m = dict(shared)
        m["x"] = np.concatenate([x_prompt[4 * c:4 * c + 4], x_sample[2 * c:2 * c + 2]], axis=0)
        m["mem"] = np.concatenate([mem_prompt[4 * c:4 * c + 4], mem_sample[2 * c:2 * c + 2]], axis=0)
        in_maps.append(m)
    res = run_bass_kernel_spmd(nc, in_maps, core_ids=list(range(NCORES)))
    y_prompt = np.empty((32, S, D), np.float32)
    y_sample = np.empty((16, S, D), np.float32)
    for c in range(NCORES):
        o = res.results[c]["out"]
        y_prompt[4 * c:4 * c + 4] = o[0:4]
        y_sample[2 * c:2 * c + 2] = o[4:6]
    return (y_prompt, y_sample)
```

```python
import numpy as np
from contextlib import ExitStack

import concourse.bass as bass
import concourse.mybir as mybir
from concourse.bass_utils import run_bass_kernel_spmd

F32 = mybir.dt.float32
BF16 = mybir.dt.bfloat16
AF = mybir.ActivationFunctionType
ALU = mybir.AluOpType

NCORES = 8
NSEQ = 6
S = 2048
D = 1024
DIN = 11264
OFF = dict(aq=0, ak=1536, av=3072, ag=4608, bq=5120, bk=5632, bv=6144, bg=6656, mq=7168, mg=7680,
           mer=8192)
GROUP_DIL = (1, 4, 16)
RMS_EPS = 1e-6
ENGS = ("sp", "act", "pool", "dve", "pe")


class Op:
    __slots__ = ("eng", "emit", "deps", "signal", "sem", "val", "dma_key", "is_dma", "idx")

    def __init__(self, eng, emit, dma_key):
        self.eng = eng
        self.emit = emit
        self.deps = set()
        self.signal = False
        self.sem = None
        self.val = 0
        self.dma_key = dma_key
        self.is_dma = dma_key is not None


class Planner:
    def __init__(self):
        self.ops = {e: [] for e in ENGS}
        self.last_w = {}
        self.readers = {}

    def add(self, eng, emit, reads=(), writes=(), dma_key=None):
        op = Op(eng, emit, dma_key)
        deps = op.deps
        for k in reads:
            w = self.last_w.get(k)
            if w is not None:
                deps.add(w)
        for k in writes:
            w = self.last_w.get(k)
            if w is not None:
                deps.add(w)
            rd = self.readers.get(k)
            if rd:
                deps.update(rd.values())
        for k in reads:
            rd = self.readers.setdefault(k, {})
            if op.is_dma:
                rd[("dma", id(op))] = op
            else:
                rd[eng] = op
        for k in writes:
            self.last_w[k] = op
            self.readers[k] = {}
        if eng == "pe":
            op.deps = {d for d in deps if d.eng != "pe" or d.is_dma}
        op.idx = len(self.ops[eng])
        self.ops[eng].append(op)
        return op

    def finalize(self):
        for e in ENGS:
            for op in self.ops[e]:
                for d in op.deps:
                    d.signal = True
        cnt = {e: 0 for e in ENGS}
        dcnt = {}
        for e in ENGS:
            for op in self.ops[e]:
                if not op.signal:
                    continue
                if op.is_dma:
                    dcnt[op.dma_key] = dcnt.get(op.dma_key, 0) + 16
                    op.sem = ("dma", op.dma_key)
                    op.val = dcnt[op.dma_key]
                else:
                    cnt[e] += 1
                    op.sem = ("eng", e)
                    op.val = cnt[e]
        return sorted({op.sem for e in ENGS for op in self.ops[e] if op.signal}, key=str)

    def emit_engine(self, e, engine, sems):
        waited = {}
        for op in self.ops[e]:
            need = {}
            for d in op.deps:
                if d.val > need.get(d.sem, 0):
                    need[d.sem] = d.val
            for sk, v in need.items():
                if v > waited.get(sk, 0):
                    engine.wait_ge(sems[sk], v)
                    waited[sk] = v
            ins = op.emit(engine)
            if op.signal:
                ins.then_inc(sems[op.sem], 16 if op.is_dma else 1)


class _DummyPlanner:
    last_w = {}

    def add(self, *a, **k):
        return None


MASK_ENG = "dve"
VEVAC_ENG = "dve"
BVEVAC_ENG = "dve"
LOOK = 3


def MM(out, lhsT, rhs, start, stop):
    return lambda e: e.matmul(out, lhsT=lhsT, rhs=rhs, start=start, stop=stop)


def ACTF(out, in_, func, **kw):
    return lambda e: e.activation(out=out, in_=in_, func=func, **kw)


def TT(out, in0, in1, op):
    return lambda e: e.tensor_tensor(out=out, in0=in0, in1=in1, op=op)


def STT(out, in0, scalar, in1, op0, op1):
    return lambda e: e.scalar_tensor_tensor(out=out, in0=in0, scalar=scalar, in1=in1, op0=op0, op1=op1)


def TS(out, in0, s1, s2, op0, op1):
    return lambda e: e.tensor_scalar(out=out, in0=in0, scalar1=s1, scalar2=s2, op0=op0, op1=op1)


def CP(out, in_):
    return lambda e: e.tensor_copy(out=out, in_=in_)


def DMA(out, in_):
    return lambda e: e.dma_start(out=out, in_=in_)


def RCP(out, in_):
    return lambda e: e.reciprocal_approx_fast(out=out, in_=in_)


def MSET(ap, v):
    return lambda e: e.memset(ap, v)


def _ea_table():
    n_heads = 24
    slopes = np.exp2(-8.0 * np.arange(1, n_heads + 1, dtype=np.float64) / n_heads).reshape(3, 8)
    ea = np.zeros((12, 128, 4, 128), np.float32)
    p = np.arange(128)
    kp = (p % 64)[:, None, None]
    i = np.arange(4)[None, :, None]
    c = np.arange(128)[None, None, :]
    d = 64 * (i - 1) + kp - c
    valid = np.abs(d) <= 64
    for g in range(3):
        for hp in range(4):
            sl = slopes[g, 2 * hp + (p // 64)][:, None, None]
            v = np.exp(-sl * GROUP_DIL[g] * np.abs(d))
            ea[g * 4 + hp] = np.where(valid, v, 0.0).astype(np.float32)
    return ea.reshape(12, 128, 512)


def _bias_b_gather(rpb):
    rpb = np.asarray(rpb, np.float32).reshape(8, 15, 31)
    p = np.arange(128)
    kc = (p % 64)[:, None]
    c = np.arange(64)[None, :]
    sc = np.clip(c - 8, 0, 48)
    valid = (kc >= sc) & (kc < sc + 16)
    dc_idx = np.clip(kc - c + 15, 0, 30)
    out = np.empty((4, 128, 15, 64), np.float32)
    for hp in range(4):
        head = 2 * hp + (p // 64)
        g = rpb[head[:, None, None], np.arange(15)[None, :, None], dc_idx[:, None, :]]
        out[hp] = np.where(valid[:, None, :], g, np.float32(-30000.0))
    return out.reshape(4, 128, 960)


def _const_mats():
    ident = np.eye(128, dtype=np.float32)
    bones = np.zeros((128, 128), np.float32)
    bones[:64, :64] = 1.0
    bones[64:, 64:] = 1.0
    ones = np.ones((128, 128), np.float32)
    return np.stack([ident, bones, ones])


def build(nseq=NSEQ):
    nc = bass.Bass("TRN2", target_bir_lowering=False)
    x_d = nc.dram_tensor("x", [nseq, S, D], F32, kind="ExternalInput").ap()
    mem_d = nc.dram_tensor("mem", [nseq, 256, D], F32, kind="ExternalInput").ap()
    w_in_d = nc.dram_tensor("w_in", [D, DIN], F32, kind="ExternalInput").ap()
    w_kv_d = nc.dram_tensor("w_kv", [D, 1024], F32, kind="ExternalInput").ap()
    w_pa_d = nc.dram_tensor("w_pa", [512, D], F32, kind="ExternalInput").ap()
    w_pb_d = nc.dram_tensor("w_pb", [512, D], F32, kind="ExternalInput").ap()
    w_pm_d = nc.dram_tensor("w_pm", [512, D], F32, kind="ExternalInput").ap()
    w_out_d = nc.dram_tensor("w_out", [D, D], F32, kind="ExternalInput").ap()
    gains_d = nc.dram_tensor("gains", [3, D], F32, kind="ExternalInput").ap()
    ea_d = nc.dram_tensor("ea", [12, 128, 512], F32, kind="ExternalInput").ap()
    bb_d = nc.dram_tensor("bb", [4, 128, 960], F32, kind="ExternalInput").ap()
    cm_d = nc.dram_tensor("cm", [3, 128, 128], F32, kind="ExternalInput").ap()
    g2c_d = nc.dram_tensor("g2c", [128, 8], F32, kind="ExternalInput").ap()
    out_d = nc.dram_tensor("out", [nseq, S, D], F32, kind="ExternalOutput").ap()

    P = None
    collect = True
    es = ExitStack()

    def sb(name, shape, dt):
        return es.enter_context(nc.sbuf_tensor(name, shape, dt))

    NWT, NWP, NXS = 5, 6, 4
    hT = sb("hT", [128, 8, S], BF16)
    gated = [sb("gated%d" % b, [128, 4, S], BF16) for b in range(3)]
    arena = sb("arena", [128, 26624], BF16)
    EAt = [sb("EA%d" % i, [128, 512], BF16) for i in range(2)]
    EB = sb("EB", [128, 4, 960], BF16)
    wt = [sb("wt%d" % i, [128, 8, 128], BF16) for i in range(NWT)]
    wp = [sb("wp%d" % i, [128, 4, 128], BF16) for i in range(NWP)]
    xs = [sb("xs%d" % i, [128, D], F32) for i in range(NXS)]
    hb = [sb("hb%d" % i, [128, D], BF16) for i in range(2)]
    gbc = {i: sb("gbc%d" % i, [128, D], F32) for i in (0, 2)}
    g2col = sb("g2col", [128, 8], F32)
    wout = sb("wout", [128, 8, D], BF16)
    mkT = sb("mkT", [128, 4, 256], BF16)
    MV = sb("MV", [128, 2, 512], BF16)
    ident = sb("ident", [128, 128], BF16)
    bones = sb("bones", [128, 128], BF16)
    ones = sb("ones", [128, 128], BF16)
    ssq = sb("ssq", [128, 32], F32)
    ms = sb("ms", [128, 32], F32)
    rstd = sb("rstd", [128, 32], F32)
    mhalf = sb("mhalf", [128, 1], F32)
    dumA = sb("dumA", [128, 2], F32)
    dumD = sb("dumD", [128, 4], F32)

    kblk = arena[:, 0:4096].rearrange("p (b k) -> p b k", k=128)
    vblkT = arena[:, 4096:8192].rearrange("p (b k) -> p b k", k=128)
    BV = arena[:, 8192:12288].rearrange("p (b k) -> p b k", k=128)
    ND = arena[:, 12288:20480].bitcast(F32).rearrange("p (t n) -> p t n", t=2)
    qT = arena[:, 20480:22528]
    PP = [arena[:, 22528 + i * 512: 22528 + (i + 1) * 512] for i in range(4)]
    sg = arena[:, 24576:26624]
    mergedT = arena[:, 0:16384].rearrange("p (c n) -> p c n", c=8)
    th = [arena[:, 16384 + i * 512: 16384 + (i + 1) * 512] for i in range(3)]
    acc = [arena[:, 17920 + i * 1024: 17920 + (i + 1) * 1024].bitcast(F32) for i in range(2)]
    tmpb = [arena[:, 19968 + i * 1024: 19968 + (i + 1) * 1024].bitcast(F32) for i in range(2)]
    memhT = arena[:, 22016:24064].rearrange("p (k n) -> p k n", k=8)

    PS = [es.enter_context(nc.psum_tensor("ps%d" % i, [128, 1024], F32)) for i in range(4)]

    def bank(t, h):
        return PS[t][:, h * 512:(h + 1) * 512]

    pj = [bank(0, 0), bank(0, 1)]
    pr = [bank(1, 0), bank(1, 1), bank(2, 0), bank(2, 1)]
    NPR = 4
    ptb = [bank(3, 0).bitcast(BF16), bank(3, 1).bitcast(BF16)]
    KPJ = [("pj", 0), ("pj", 1)]
    KPR = [("pr", i) for i in range(NPR)]
    KPT = [("ptb", 0), ("ptb", 1)]
    junk = PS[2][:, :]
    KJUNK = [KPR[2], KPR[3]]

    GU = [("ar_u", "act"), ("ar_u", "dve"), ("ar_u", "pool")]
    GT = [("ar_t", "act"), ("ar_t", "dve"), ("ar_t", "pool")]
    KID, KBONES, KONES = ("const", 0), ("const", 1), ("const", 2)

    ctr = {}
    wlist = []

    def gen():
        ctr.clear()

        def nxt(name, n):
            v = ctr.get(name, 0)
            ctr[name] = v + 1
            return v % n

        for i, t in enumerate((ident, bones, ones)):
            P.add("pool", DMA(t[:, :], cm_d[i]), writes=[("const", i)], dma_key=("const", i))
        P.add("pool", MSET(mhalf[:, :], -0.5), writes=[("mhalf",)])
        for i in (0, 2):
            P.add("sp", DMA(gbc[i][:, :], gains_d[i:i + 1, :].broadcast_to([128, D])),
                  writes=[("gbc", i)], dma_key=("gbc", i))
        P.add("sp", DMA(g2col[:, :], g2c_d[:, :]), writes=[("g2col",)], dma_key=("g2col",))
        for kc in range(8):
            P.add("pool", DMA(wout[:, kc, :], w_out_d[kc * 128:(kc + 1) * 128, :]),
                  writes=[("wout", kc)], dma_key=("wout", kc))
        for hp in range(4):
            sl = nxt("xs", NXS)
            P.add("sp", DMA(xs[sl][:, 0:960], bb_d[hp]), writes=[("xs", sl)], dma_key=("xs", sl))
            P.add("act", ACTF(EB[:, hp, :], xs[sl][:, 0:960], AF.Exp), reads=[("xs", sl)], writes=[("EB", hp)])

        WLOOK = 2

        def _issue_w(i):
            sl = i % NWT
            P.add("pool", DMA(wt[sl][:, :, :], wlist[i].rearrange("(kc p) c -> p kc c", p=128)),
                  writes=[("wt", sl)], dma_key=("wt", sl))

        def load_w(src_ap):
            i = ctr.get("wreq", 0)
            ctr["wreq"] = i + 1
            if collect:
                wlist.append(src_ap)
                return i % NWT
            while ctr.get("wiss", 0) <= min(i + WLOOK, len(wlist) - 1):
                _issue_w(ctr.get("wiss", 0))
                ctr["wiss"] = ctr.get("wiss", 0) + 1
            return i % NWT

        def load_wp(src_ap):
            sl = nxt("wp", NWP)
            P.add("pool", DMA(wp[sl][:, :, :], src_ap.rearrange("(kc p) c -> p kc c", p=128)),
                  writes=[("wp", sl)], dma_key=("wp", sl))
            return sl

        def rstd_col(col, src_key):
            P.add("dve", TS(ms[:, col:col + 1], ssq[:, col:col + 1], 1.0 / D, RMS_EPS, ALU.mult, ALU.add),
                  reads=[("ssq", col)], writes=[("ms", col)])
            P.add("pool", TT(rstd[:, col:col + 1], ms[:, col:col + 1], mhalf[:, 0:1], ALU.pow),
                  reads=[("ms", col), ("mhalf",)], writes=[("rstd", col)])

        def norm_a(src_rows):
            sl = nxt("xs", NXS)
            col = nxt("col", 32)
            P.add("sp", DMA(xs[sl][:, :], src_rows), writes=[("xs", sl)], dma_key=("xs", sl))
            P.add("act", ACTF(junk, xs[sl][:, :], AF.Square, accum_out=ssq[:, col:col + 1]),
                  reads=[("xs", sl)], writes=[("ssq", col)] + KJUNK)
            rstd_col(col, None)
            return sl, col

        def norm_b1(st, gi):
            sl, col = st
            hs = nxt("hb", 2)
            if gi is not None:
                P.add("dve", STT(hb[hs][:, :], xs[sl][:, :], rstd[:, col:col + 1], gbc[gi][:, :], ALU.mult, ALU.mult),
                      reads=[("xs", sl), ("rstd", col), ("gbc", gi)], writes=[("hb", hs)])
            else:
                P.add("dve", TS(hb[hs][:, :], xs[sl][:, :], rstd[:, col:col + 1], None, ALU.mult, ALU.bypass),
                      reads=[("xs", sl), ("rstd", col)], writes=[("hb", hs)])
            return hs

        def norm_b2(hs):
            pt = nxt("ptb", 2)
            for kc in range(8):
                P.add("pe", lambda e, kc=kc, pt=pt, hs=hs: e.transpose(
                    out=ptb[pt][:, kc * 128:(kc + 1) * 128], in_=hb[hs][:, kc * 128:(kc + 1) * 128], identity=ident[:, :]),
                    reads=[("hb", hs), KID], writes=[KPT[pt]])
            return pt

        def norm_b(st, gi):
            return norm_b2(norm_b1(st, gi))

        def norm_c(pt, gi, dstT, dst_cols, dst_keys, extra_r=()):
            src = ptb[pt].rearrange("p (k n) -> p k n", k=8)
            if gi is not None:
                P.add("act", ACTF(dstT[:, :, dst_cols], src, AF.Copy), reads=[KPT[pt]] + list(extra_r), writes=list(dst_keys))
            else:
                P.add("dve", TT(dstT[:, :, dst_cols], src, g2col[:, :].unsqueeze(2).to_broadcast([128, 8, 128]), ALU.mult),
                      reads=[KPT[pt], ("g2col",)] + list(extra_r), writes=list(dst_keys))

        def run_pipes(pipes, n):
            depth = max(len(p) for p in pipes)
            states = [[{} for _ in range(n)] for _ in pipes]
            for i in range(n + depth - 1):
                for pi, p in enumerate(pipes):
                    for k in reversed(range(len(p))):
                        tt = i - k
                        if 0 <= tt < n:
                            p[k](tt, states[pi][tt])

        def inproj_tile(w_sl, tc):
            b = nxt("pj", 2)
            for kc in range(8):
                P.add("pe", MM(pj[b], wt[w_sl][:, kc, :], hT[:, kc, tc * 512:(tc + 1) * 512], kc == 0, kc == 7),
                      reads=[("wt", w_sl), ("hT",)], writes=[KPJ[b]])
            return b

        def fence(phase_keys):
            P.add("act", ACTF(dumA[:, 0:1], mhalf[:, 0:1], AF.Copy), reads=[("mhalf",)], writes=[phase_keys[0]])
            P.add("dve", MSET(dumD[:, 0:1], 0.0), writes=[phase_keys[1]])

        def evac_q(b, tc, dil):
            src = pj[b]
            if dil == 1:
                o = qT[:, tc * 512:(tc + 1) * 512]
                i_ = src
            else:
                n = 512 // dil
                o = qT.rearrange("p (r i) -> p r i", r=dil)[:, :, tc * n:(tc + 1) * n]
                i_ = src.rearrange("p (i r) -> p r i", r=dil)
            P.add("dve", CP(o, i_), reads=[KPJ[b]] + GU, writes=[("qT", tc)])

        def evac_blk(dst, dkey, b, tc, dil, eng="dve"):
            for hh in range(2):
                rows = slice(hh * 64, hh * 64 + 64)
                cs = slice(hh * 64, hh * 64 + 64)
                src = pj[b][rows, :]
                if dil == 1:
                    o = dst[rows, tc * 8:(tc + 1) * 8, cs]
                    i_ = src.rearrange("p (b k) -> p b k", k=64)
                elif dil == 4:
                    o = dst.rearrange("p (r b) k -> p r b k", r=4)[rows, :, 2 * tc:2 * tc + 2, cs]
                    i_ = src.rearrange("p (b k r) -> p r b k", r=4, k=64)
                else:
                    c0 = hh * 64 + 32 * (tc % 2)
                    o = dst.rearrange("p (r b) k -> p r b k", r=16)[rows, :, tc // 2, c0:c0 + 32]
                    i_ = src.rearrange("p (i r) -> p r i", r=16)
                if eng == "dve":
                    P.add("dve", CP(o, i_), reads=[KPJ[b]] + GU, writes=[(dkey, tc, hh)])
                else:
                    P.add("act", ACTF(o, i_, AF.Copy), reads=[KPJ[b]] + GU, writes=[(dkey, tc, hh)])

        KQ = [("qT", tc) for tc in range(4)]
        KK = [("kblk", tc, hh) for tc in range(4) for hh in range(2)]
        KVT = [("vblkT", tc, hh) for tc in range(4) for hh in range(2)]
        KBV = [("BV", q) for q in range(4)]
        KSG = [("sg", tc) for tc in range(4)]

        def build_bv():
            for q in range(4):
                pt = nxt("ptb", 2)
                for j in range(8):
                    blk = q * 8 + j
                    P.add("pe", lambda e, blk=blk, j=j, pt=pt: e.transpose(
                        out=ptb[pt][:, j * 128:(j + 1) * 128], in_=vblkT[:, blk, :], identity=ident[:, :]),
                        reads=KVT + [KID] + GU, writes=[KPT[pt]])
                P.add("act", ACTF(BV[:, q * 8:(q + 1) * 8, :], ptb[pt].rearrange("p (b k) -> p b k", k=128), AF.Copy),
                      reads=[KPT[pt]] + GU, writes=[("BV", q)])

        def v_chunks(vc0, dil):
            box = {}

            def chunk(tc):
                if "w" not in box:
                    box["w"] = load_w(w_in_d[:, vc0:vc0 + 128])
                b = inproj_tile(box["w"], tc)
                evac_blk(vblkT, "vblkT", b, tc, dil, eng=VEVAC_ENG)

            return [lambda tc=tc: chunk(tc) for tc in range(4)]

        def qk_inproj(qc0, kc0, dil):
            for (c0, kind) in ((qc0, "q"), (kc0, "k")):
                if kind == "k":
                    build_bv()
                w_sl = load_w(w_in_d[:, c0:c0 + 128])
                for tc in range(4):
                    b = inproj_tile(w_sl, tc)
                    if kind == "q":
                        evac_q(b, tc, dil)
                        flush_fin(1)
                    else:
                        evac_blk(kblk, "kblk", b, tc, dil)
                        if tc == 0:
                            flush_fin()

        def gate_chunks(c0, tslots=(0, 1)):
            box = {}

            def chunk(tc):
                if "w" not in box:
                    box["w"] = load_w(w_in_d[:, c0:c0 + 128])
                b = inproj_tile(box["w"], tc)
                t = tslots[tc % len(tslots)]
                P.add("act", ACTF(EAt[t][:, :], pj[b], AF.Tanh, scale=0.5), reads=[KPJ[b]], writes=[("EA", t)])
                P.add("dve", STT(sg[:, tc * 512:(tc + 1) * 512], EAt[t][:, :], 1.0, pj[b], ALU.add, ALU.mult),
                      reads=[KPJ[b], ("EA", t)] + GU, writes=[("sg", tc)])

            return [lambda tc=tc: chunk(tc) for tc in range(4)]

        def gate_inproj(c0, tslots=(0, 1)):
            for f in gate_chunks(c0, tslots):
                f()

        def nd_keys():
            return [k for k in P.last_w.keys() if isinstance(k, tuple) and k and k[0] == "ND"]

        def pipeline(st1, st2, look, fillers=(), every=1, pre=(), mid=()):
            n = len(st1)
            fl = list(fillers)
            for f in pre:
                f()
            for i in range(min(look, n)):
                st1[i]()
            for f in mid:
                f()
            for i in range(n):
                if i + look < n:
                    st1[i + look]()
                if fl and i % every == every - 1:
                    fl.pop(0)()
                st2[i]()
            while fl:
                fl.pop(0)()

        def load_ea(g, hp):
            ea = nxt("ea", 2)
            P.add("pool", DMA(EAt[ea][:, :], ea_d[g * 4 + hp]), writes=[("EA", ea)], dma_key=("EA", ea))
            return ea

        def mask_mult(ap, tab, rkeys, wkey):
            P.add(MASK_ENG, TT(ap, ap, tab, ALU.mult), reads=rkeys + GU, writes=[wkey])

        def attn_banded(g, hp, first, ea, pre=(), mid=()):
            dil = GROUP_DIL[g]
            nb = 32 // dil
            st1, st2 = [], []
            for sidx in range(dil):
                for n2 in range(nb // 2):
                    st = {}

                    def s1(sidx=sidx, n2=n2, st=st):
                        B0 = sidx * nb
                        iv = [i for i in range(4) if 0 <= 2 * n2 - 1 + i < nb]
                        i0, i1 = iv[0], iv[-1]
                        sb_ = nxt("pr", NPR)
                        q0 = (B0 + 2 * n2) * 64
                        for i in iv:
                            jl = 2 * n2 - 1 + i
                            P.add("pe", MM(pr[sb_][:, i * 128:(i + 1) * 128], kblk[:, B0 + jl, :], qT[:, q0:q0 + 128], True, True),
                                  reads=KQ + KK + GU, writes=[KPR[sb_]])
                        t = nxt("pp", 4)
                        cs = slice(i0 * 128, (i1 + 1) * 128)
                        P.add("act", ACTF(PP[t][:, cs], pr[sb_][:, cs], AF.Exp, scale=0.125),
                              reads=[KPR[sb_]] + GU, writes=[("PP", t)])
                        mask_mult(PP[t][:, cs], EAt[ea][:, cs], [("PP", t), ("EA", ea)], ("PP", t))
                        st.update(t=t, iv=iv, B0=B0, q0=q0)

                    def s2(sidx=sidx, n2=n2, st=st):
                        t, iv, B0, q0 = st["t"], st["iv"], st["B0"], st["q0"]
                        i0, i1 = iv[0], iv[-1]
                        ob = nxt("pr", NPR)
                        for i in iv:
                            jl = 2 * n2 - 1 + i
                            P.add("pe", MM(pr[ob][:, 0:128], BV[:, B0 + jl, :], PP[t][:, i * 128:(i + 1) * 128], i == i0, i == i1),
                                  reads=[("PP", t)] + KBV + GU, writes=[KPR[ob]])
                        for i in iv:
                            P.add("pe", MM(pr[ob][:, 128:256], bones[:, :], PP[t][:, i * 128:(i + 1) * 128], i == i0, i == i1),
                                  reads=[("PP", t), KBONES] + GU, writes=[KPR[ob]])
                        if dil == 1:
                            dst = ND[:, :, q0:q0 + 128]
                        elif dil == 4:
                            t0 = sidx + 4 * 128 * n2
                            dst = ND[:, :, t0:t0 + 4 * 127 + 1:4]
                        else:
                            dst = ND[:, :, sidx:S:16]
                        src = pr[ob][:, 0:256].rearrange("p (t n) -> p t n", t=2)
                        ndk = [("ND", g, sidx, n2)]
                        if first:
                            P.add("act", ACTF(dst, src, AF.Copy), reads=[KPR[ob], ("NDfree",)] + GU, writes=ndk)
                        else:
                            P.add("dve", TT(dst, src, dst, ALU.add), reads=[KPR[ob], ("NDall",)] + GU, writes=ndk)

                    st1.append(s1)
                    st2.append(s2)
            pipeline(st1, st2, LOOK, pre=pre, mid=mid)

        def nd_barrier():
            P.add("dve", MSET(dumD[:, 1:2], 0.0), reads=nd_keys(), writes=[("NDall",)])

        pend = []

        def finalize_unit(dst_gated, gkey):
            for h in range(2):
                c = slice(h * 1024, (h + 1) * 1024)
                P.add("act", ACTF(ND[:, 1, c], ND[:, 1, c], AF.Ln), reads=[("NDall",)] + GU, writes=[("NDl", h)])
                P.add("act", ACTF(ND[:, 1, c], ND[:, 1, c], AF.Exp, scale=-1.0), reads=[("NDl", h)] + GU, writes=[("NDr", h)])

            def op_tt(h):
                c = slice(h * 1024, (h + 1) * 1024)
                P.add("dve", TT(ND[:, 0, c], ND[:, 0, c], ND[:, 1, c], ALU.mult), reads=[("NDr", h)] + GU, writes=[("NDn", h)])

            def op_stt(h):
                c = slice(h * 1024, (h + 1) * 1024)
                P.add("dve", STT(dst_gated[:, c], ND[:, 0, c], 0.5, sg[:, c], ALU.mult, ALU.mult),
                      reads=[("NDn", h)] + KSG + GU, writes=[gkey, ("NDg", h)])
                if h == 1:
                    P.add("dve", MSET(dumD[:, 2:3], 0.0), reads=[("NDg", 0), ("NDg", 1)], writes=[("NDfree",)])

            pend.extend([lambda: op_tt(0), lambda: op_stt(0), lambda: op_tt(1), lambda: op_stt(1)])

        def flush_fin(n=None):
            while pend and (n is None or n > 0):
                pend.pop(0)()
                if n is not None:
                    n -= 1

        def attn_nbr(hp, pre=(), mid=()):
            st1, st2 = [], []
            obs = {}
            for r in range(32):
                st = {}

                def s1(r=r, st=st):
                    r0 = min(max(r - 4, 0), 24)
                    sb_ = nxt("pr", NPR)
                    for i in range(8):
                        P.add("pe", MM(pr[sb_][:, i * 64:(i + 1) * 64], kblk[:, r0 + i, :], qT[:, r * 64:(r + 1) * 64], True, True),
                              reads=KQ + KK + GU, writes=[KPR[sb_]])
                    t = nxt("pp", 4)
                    P.add("act", ACTF(PP[t][:, :], pr[sb_], AF.Exp, scale=0.125), reads=[KPR[sb_]] + GU, writes=[("PP", t)])
                    e0 = (r0 - r + 7) * 64
                    mask_mult(PP[t][:, :], EB[:, hp, e0:e0 + 512], [("PP", t), ("EB", hp)], ("PP", t))
                    st.update(t=t, r0=r0)

                def s2(r=r, st=st):
                    t, r0 = st["t"], st["r0"]
                    if r % 4 == 0:
                        obs["ob"] = nxt("pr", NPR)
                    ob = obs["ob"]
                    o0 = (r % 4) * 64
                    for i in range(8):
                        P.add("pe", MM(pr[ob][:, o0:o0 + 64], BV[:, r0 + i, :], PP[t][:, i * 64:(i + 1) * 64], i == 0, i == 7),
                              reads=[("PP", t)] + KBV + GU, writes=[KPR[ob]])
                    for i in range(8):
                        P.add("pe", MM(pr[ob][:, 256 + o0:256 + o0 + 64], bones[:, :], PP[t][:, i * 64:(i + 1) * 64], i == 0, i == 7),
                              reads=[("PP", t), KBONES] + GU, writes=[KPR[ob]])
                    if r % 4 == 3:
                        q0 = (r - 3) * 64
                        P.add("dve", CP(ND[:, :, q0:q0 + 256], pr[ob].rearrange("p (t n) -> p t n", t=2)),
                              reads=[KPR[ob], ("NDfree",)] + GU, writes=[("ND", "b", r)])

                st1.append(s1)
                st2.append(s2)
            pipeline(st1, st2, LOOK, pre=pre, mid=mid)

        def attn_mem(h, fillers=()):
            st1, st2 = [], []
            for qc in range(4):
                st = {}

                def s1(qc=qc, st=st):
                    tl = []
                    for kt in range(2):
                        P.add("pe", MM(pr[kt], mkT[:, h, kt * 128:(kt + 1) * 128], qT[:, qc * 512:(qc + 1) * 512], True, True),
                              reads=KQ + [("mkT", h)] + GU, writes=[KPR[kt]])
                        t = 2 * (qc % 2) + kt
                        tl.append(t)
                        P.add("act", ACTF(PP[t][:, :], pr[kt], AF.Exp, scale=float(128 ** -0.5)),
                              reads=[KPR[kt]] + GU, writes=[("PP", t)])
                    st["tl"] = tl

                def s2(qc=qc, st=st):
                    tl = st["tl"]
                    for kt in range(2):
                        P.add("pe", MM(pr[2], MV[:, kt, h * 128:(h + 1) * 128], PP[tl[kt]][:, :], kt == 0, kt == 1),
                              reads=[("PP", tl[kt]), ("MV",)] + GU, writes=[KPR[2]])
                    for kt in range(2):
                        P.add("pe", MM(pr[3], ones[:, :], PP[tl[kt]][:, :], kt == 0, kt == 1),
                              reads=[("PP", tl[kt]), KONES] + GU, writes=[KPR[3]])
                    for ob in range(2):
                        P.add("dve", CP(ND[:, ob, qc * 512:(qc + 1) * 512], pr[2 + ob]),
                              reads=[KPR[2 + ob], ("NDfree",)] + GU, writes=[("ND", "m", qc, ob)])

                st1.append(s1)
                st2.append(s2)
            pipeline(st1, st2, 1, fillers, 1)

        def mem_branch(s):
            sts = [norm_a(mem_d[s, t * 128:(t + 1) * 128, :]) for t in range(2)]
            pts = [norm_b(sts[t], None) for t in range(2)]
            for t in range(2):
                norm_c(pts[t], None, memhT, slice(t * 128, (t + 1) * 128), [("memhT", t)], extra_r=GT)
            rk = [("memhT", 0), ("memhT", 1)] + GT
            for h in range(4):
                w_sl = load_w(w_kv_d[:, h * 128:(h + 1) * 128])
                b = nxt("pj", 2)
                for kc in range(8):
                    P.add("pe", MM(pj[b][:, 0:256], wt[w_sl][:, kc, :], memhT[:, kc, :], kc == 0, kc == 7),
                          reads=[("wt", w_sl)] + rk, writes=[KPJ[b]])
                P.add("dve", CP(mkT[:, h, :], pj[b][:, 0:256]), reads=[KPJ[b]], writes=[("mkT", h)])
            for h in range(4):
                w_sl = load_w(w_kv_d[:, 512 + h * 128:512 + (h + 1) * 128])
                b = nxt("pj", 2)
                for kt in range(2):
                    for kc in range(8):
                        P.add("pe", MM(pj[b][:, kt * 128:(kt + 1) * 128], memhT[:, kc, kt * 128:(kt + 1) * 128],
                                       wt[w_sl][:, kc, :], kc == 0, kc == 7),
                              reads=[("wt", w_sl)] + rk, writes=[KPJ[b]])
                P.add("dve", CP(MV[:, :, h * 128:(h + 1) * 128], pj[b][:, 0:256].rearrange("p (t n) -> p t n", t=2)),
                      reads=[KPJ[b]], writes=[("MV",)])

        def zero_blocks():
            P.add("pool", MSET(arena[:, 0:8192], 0.0), reads=GU, writes=[GT[2]] + KK + KVT)

        def tail(s):
            fence(GU)
            wps = (w_pa_d, w_pb_d, w_pm_d)
            for c in range(8):
                wg = [load_w(w_in_d[:, OFF["mer"] + b * 1024 + c * 128: OFF["mer"] + b * 1024 + (c + 1) * 128])
                      for b in range(3)]
                if c == 0:
                    wpn = [load_wp(wps[b][:, 0:128]) for b in range(3)]
                wpl = wpn
                if c + 1 < 8:
                    wpn = [load_wp(wps[b][:, (c + 1) * 128:(c + 2) * 128]) for b in range(3)]
                for tc in range(4):
                    a = nxt("acc", 2)
                    for b in range(3):
                        pb_ = inproj_tile(wg[b], tc)
                        t = nxt("th", 3)
                        P.add("act", ACTF(th[t][:, :], pj[pb_], AF.Tanh, scale=0.5), reads=[KPJ[pb_]] + GT, writes=[("th", t)])
                        sb_ = nxt("pr", NPR)
                        for kc in range(4):
                            P.add("pe", MM(pr[sb_], wp[wpl[b]][:, kc, :], gated[b][:, kc, tc * 512:(tc + 1) * 512], kc == 0, kc == 3),
                                  reads=[("wp", wpl[b]), ("gated", b)], writes=[KPR[sb_]])
                        if b == 0:
                            P.add("dve", STT(acc[a][:, :], th[t][:, :], 1.0, pr[sb_], ALU.add, ALU.mult),
                                  reads=[("th", t), KPR[sb_]] + GT, writes=[("acc", a)])
                        else:
                            m = nxt("tmp", 2)
                            P.add("dve", STT(tmpb[m][:, :], th[t][:, :], 1.0, pr[sb_], ALU.add, ALU.mult),
                                  reads=[("th", t), KPR[sb_]] + GT, writes=[("tmp", m)])
                            if b == 1:
                                P.add("dve", TT(acc[a][:, :], acc[a][:, :], tmpb[m][:, :], ALU.add),
                                      reads=[("acc", a), ("tmp", m)] + GT, writes=[("acc", a)])
                            else:
                                P.add("dve", TT(mergedT[:, c, tc * 512:(tc + 1) * 512], acc[a][:, :], tmpb[m][:, :], ALU.add),
                                      reads=[("acc", a), ("tmp", m)] + GT, writes=[("mergedT", c, tc)])

        def final_pipe(s):
            def fb(tt, st):
                sl = nxt("xs", NXS)
                col = nxt("col", 32)
                rows = slice(tt * 128, (tt + 1) * 128)
                P.add("sp", DMA(xs[sl][:, :], x_d[s, rows, :]), writes=[("xs", sl)], dma_key=("xs", sl))
                mk = [("mergedT", c, tt // 4) for c in range(8)]
                for half in range(2):
                    b = nxt("pj", 2)
                    for kc in range(8):
                        P.add("pe", MM(pj[b], mergedT[:, kc, rows], wout[:, kc, half * 512:(half + 1) * 512], kc == 0, kc == 7),
                              reads=mk + [("wout", kc)] + GT, writes=[KPJ[b]])
                    hs = slice(half * 512, (half + 1) * 512)
                    P.add("dve", STT(xs[sl][:, hs], pj[b], 0.5, xs[sl][:, hs], ALU.mult, ALU.add),
                          reads=[KPJ[b], ("xs", sl)], writes=[("xs", sl)])
                P.add("act", ACTF(junk, xs[sl][:, :], AF.Square, accum_out=ssq[:, col:col + 1]),
                      reads=[("xs", sl)], writes=[("ssq", col)] + KJUNK)
                rstd_col(col, None)
                st.update(sl=sl, col=col, rows=rows)

            def fc(tt, st):
                sl, col, rows = st["sl"], st["col"], st["rows"]
                P.add("dve", STT(xs[sl][:, :], xs[sl][:, :], rstd[:, col:col + 1], gbc[2][:, :], ALU.mult, ALU.mult),
                      reads=[("xs", sl), ("rstd", col), ("gbc", 2)], writes=[("xs", sl)])
                P.add("pool", DMA(out_d[s, rows, :], xs[sl][:, :]), reads=[("xs", sl)], writes=[("out", s, tt)],
                      dma_key=("st", sl))

            return [fb, fc]

        def phase_n_pipe(s):
            def a(tt, st):
                st["a"] = norm_a(x_d[s, tt * 128:(tt + 1) * 128, :])

            def b1(tt, st):
                st["hs"] = norm_b1(st["a"], 0)

            def b2(tt, st):
                st["pt"] = norm_b2(st["hs"])

            def c(tt, st):
                norm_c(st["pt"], 0, hT, slice(tt * 128, (tt + 1) * 128), [("hT",)])

            return [a, b1, b2, c]

        def units(s):
            fence(GT)
            zero_blocks()
            ul = []
            for hp in range(4):
                for g in range(3):
                    c = g * 512 + hp * 128
                    ul.append(("A", g, hp, OFF["aq"] + c, OFF["ak"] + c, OFF["av"] + c, GROUP_DIL[g]))
            for hp in range(4):
                c = hp * 128
                ul.append(("B", 0, hp, OFF["bq"] + c, OFF["bk"] + c, OFF["bv"] + c, 1))
            for f in v_chunks(ul[0][5], ul[0][6]):
                f()
            for i, (kind, g, hp, qc0, kc0, vc0, dil) in enumerate(ul):
                if i + 1 < len(ul):
                    vn = v_chunks(ul[i + 1][5], ul[i + 1][6])
                    pre, mid = vn[0:2], vn[2:4]
                else:
                    pre, mid = (), ()
                if kind == "A":
                    ea = load_ea(g, hp)
                    qk_inproj(qc0, kc0, dil)
                    attn_banded(g, hp, g == 0, ea, pre, mid)
                    nd_barrier()
                    if g == 2:
                        gate_inproj(OFF["ag"] + hp * 128)
                        finalize_unit(gated[0][:, hp, :], ("gated", 0))
                else:
                    qk_inproj(qc0, kc0, dil)
                    attn_nbr(hp, pre, mid)
                    nd_barrier()
                    gate_inproj(OFF["bg"] + hp * 128)
                    finalize_unit(gated[1][:, hp, :], ("gated", 1))
            for h in range(4):
                w_sl = load_w(w_in_d[:, OFF["mq"] + h * 128: OFF["mq"] + (h + 1) * 128])
                for tc in range(4):
                    b = inproj_tile(w_sl, tc)
                    evac_q(b, tc, 1)
                flush_fin()
                attn_mem(h, gate_chunks(OFF["mg"] + h * 128))
                nd_barrier()
                finalize_unit(gated[2][:, h, :], ("gated", 2))
            flush_fin()

        run_pipes([phase_n_pipe(0)], 16)
        for s in range(nseq):
            mem_branch(s)
            units(s)
            tail(s)
            pipes = [final_pipe(s)]
            if s + 1 < nseq:
                pipes.insert(0, phase_n_pipe(s + 1))
            run_pipes(pipes, 16)
        P.add("sp", lambda e: e.nop(), reads=[("out", s, tt) for s in range(nseq) for tt in range(16)])


    collect = True
    P = _DummyPlanner()
    gen()
    collect = False
    P = Planner()
    gen()

    sem_keys = P.finalize()
    sems = {}
    for i, sk in enumerate(sem_keys):
        sems[sk] = es.enter_context(nc.semaphore("sem%d" % i))
    with nc.Block() as block:
        @block.sync
        def _(e):
            P.emit_engine("sp", e, sems)

        @block.scalar
        def _(e):
            P.emit_engine("act", e, sems)

        @block.gpsimd
        def _(e):
            P.emit_engine("pool", e, sems)

        @block.vector
        def _(e):
            P.emit_engine("dve", e, sems)

        @block.tensor
        def _(e):
            P.emit_engine("pe", e, sems)
    es.close()
    return nc, P


def _shared_inputs(norm_gain, mem_norm_gain, w_in, w_mem_kv, rpb, w_proj_a, w_proj_b, w_proj_m, w_out,
                   final_norm_gain):
    f = lambda a: np.ascontiguousarray(np.asarray(a, dtype=np.float32))
    return {
        "w_in": f(w_in).reshape(D, DIN), "w_kv": f(w_mem_kv).reshape(D, 1024),
        "w_pa": f(w_proj_a).reshape(512, D), "w_pb": f(w_proj_b).reshape(512, D),
        "w_pm": f(w_proj_m).reshape(512, D), "w_out": f(w_out).reshape(D, D),
        "gains": np.stack([f(norm_gain).reshape(D), f(mem_norm_gain).reshape(D), f(final_norm_gain).reshape(D)]),
        "ea": _ea_table(), "bb": _bias_b_gather(rpb), "cm": _const_mats(),
        "g2c": np.ascontiguousarray(f(mem_norm_gain).reshape(8, 128).T),
    }


_CACHE = {}


def kernel(x_prompt, x_sample, mem_prompt, mem_sample, norm_gain, mem_norm_gain, w_in, w_mem_kv, rpb,
           w_proj_a, w_proj_b, w_proj_m, w_out, final_norm_gain):
    f = lambda a: np.ascontiguousarray(np.asarray(a, dtype=np.float32))
    x_prompt, x_sample, mem_prompt, mem_sample = f(x_prompt), f(x_sample), f(mem_prompt), f(mem_sample)
    if "nc" not in _CACHE:
        _CACHE["nc"] = build(NSEQ)[0]
    nc = _CACHE["nc"]
    shared = _shared_inputs(norm_gain, mem_norm_gain, w_in, w_mem_kv, rpb, w_proj_a, w_proj_b, w_proj_m, w_out,
                            final_norm_gain)
    in_maps = []
    for c in range(NCORES):
        m = dict(shared)
        m["x"] = np.concatenate([x_prompt[4 * c:4 * c + 4], x_sample[2 * c:2 * c + 2]], axis=0)
        m["mem"] = np.concatenate([mem_prompt[4 * c:4 * c + 4], mem_sample[2 * c:2 * c + 2]], axis=0)
        in_maps.append(m)
    res = run_bass_kernel_spmd(nc, in_maps, core_ids=list(range(NCORES)))
    y_prompt = np.empty((32, S, D), np.float32)
    y_sample = np.empty((16, S, D), np.float32)
    for c in range(NCORES):
        o = res.results[c]["out"]
        y_prompt[4 * c:4 * c + 4] = o[0:4]
        y_sample[2 * c:2 * c + 2] = o[4:6]
    return (y_prompt, y_sample)
```
